# Optimizing a Trainium2 kernel written in Bass

```python
import jax, jax.numpy as jnp
from jax import lax
import numpy as np

D_MODEL = 1024
BATCH = 16
SEQ = 4096
DEPTH = 1

D_MIX = D_MODEL
D_CONF = D_MIX // 2
D_SC = D_MIX - D_CONF
CONF_HEADS = 8
SC_HEADS = 8
CONF_KERNEL = 31
SC_KERNEL = 3
D_FF = 4 * D_MODEL
D_IN = 2 * D_CONF + 3 * D_SC
N_MOD = 6
EPS = 1e-6

kernel_name = "hybrid_conformer_shortconv_adaln_block"


def rms_norm(x, gain=None):
    xf = x.astype(jnp.float32)
    y = xf * lax.rsqrt(jnp.mean(xf * xf, axis=-1, keepdims=True) + EPS)
    if gain is not None:
        y = y * gain.astype(jnp.float32)
    return y.astype(x.dtype)


def layer_norm(x, gain, bias):
    xf = x.astype(jnp.float32)
    mu = jnp.mean(xf, axis=-1, keepdims=True)
    var = jnp.mean(jnp.square(xf - mu), axis=-1, keepdims=True)
    y = (xf - mu) * lax.rsqrt(var + EPS) * gain.astype(jnp.float32) + bias.astype(jnp.float32)
    return y.astype(x.dtype)


def causal_depthwise_conv(u, w):
    k = w.shape[0]
    return lax.conv_general_dilated(
        u, w[:, None, :].astype(u.dtype), window_strides=(1,), padding=[(k - 1, 0)],
        dimension_numbers=("NWC", "WIO", "NWC"), feature_group_count=u.shape[-1])


def modulate(h, shift, scale):
    return h * (1.0 + scale[:, None, :]) + shift[:, None, :]


def setup_inputs(seed: int = 0) -> dict:
    key = jax.random.key(seed)
    ks = jax.random.split(key, 16)
    f = jnp.float32
    n = lambda k, shape, s: (jax.random.normal(k, shape, f) * s).astype(f)
    return {
        "x": n(ks[0], (BATCH, SEQ, D_MODEL), 1.0),
        "c": n(ks[1], (BATCH, D_MODEL), 1.0),
        "w_ada": n(ks[2], (DEPTH, D_MODEL, N_MOD * D_MODEL), 0.5 * D_MODEL ** -0.5),
        "b_ada": n(ks[3], (DEPTH, N_MOD * D_MODEL), 0.02),
        "w_in": n(ks[4], (DEPTH, D_MODEL, D_IN), D_MODEL ** -0.5),
        "conf_dw_w": n(ks[5], (DEPTH, CONF_KERNEL, D_CONF), CONF_KERNEL ** -0.5),
        "conf_dw_b": n(ks[6], (DEPTH, D_CONF), 0.02),
        "conf_ln_g": 1.0 + n(ks[7], (DEPTH, D_CONF), 0.02),
        "conf_ln_b": n(ks[8], (DEPTH, D_CONF), 0.02),
        "sc_conv_w": n(ks[9], (DEPTH, SC_KERNEL, D_SC), SC_KERNEL ** -0.5),
        "w_out": n(ks[10], (DEPTH, D_MIX, D_MODEL), D_MIX ** -0.5),
        "w_mlp1": n(ks[11], (DEPTH, D_MODEL, D_FF), D_MODEL ** -0.5),
        "w_mlp2": n(ks[12], (DEPTH, D_FF, D_MODEL), D_FF ** -0.5),
        "g_final": 1.0 + n(ks[13], (D_MODEL,), 0.02),
    }


def reference(x, c, w_ada, b_ada, w_in, conf_dw_w, conf_dw_b, conf_ln_g, conf_ln_b,
              sc_conv_w, w_out, w_mlp1, w_mlp2, g_final):
    dt = x.dtype
    c_act = jax.nn.silu(c)
    for l in range(DEPTH):
        mod = jnp.einsum("bd,de->be", c_act, w_ada[l]) + b_ada[l]
        sh1, sc1, g1, sh2, sc2, g2 = jnp.split(mod.astype(dt), N_MOD, axis=-1)

        h = modulate(rms_norm(x), sh1, sc1)
        proj = jnp.einsum("btd,de->bte", h, w_in[l])
        conf_val, conf_gate, sc_b, sc_c, sc_h = jnp.split(
            proj, np.cumsum([D_CONF, D_CONF, D_SC, D_SC])[:4].tolist(), axis=-1)

        a = conf_val * jax.nn.sigmoid(conf_gate)
        a = causal_depthwise_conv(a, conf_dw_w[l]) + conf_dw_b[l].astype(dt)
        a = jax.nn.silu(layer_norm(a, conf_ln_g[l], conf_ln_b[l]))

        s = sc_b * causal_depthwise_conv(sc_c * sc_h, sc_conv_w[l])

        mixed = jnp.concatenate([a, s], axis=-1)
        y = jnp.einsum("bte,ed->btd", mixed, w_out[l])
        x = x + g1[:, None, :] * y

        h = modulate(rms_norm(x), sh2, sc2)
        u = jnp.square(jax.nn.relu(jnp.einsum("btd,df->btf", h, w_mlp1[l])))
        y = jnp.einsum("btf,fd->btd", u, w_mlp2[l])
        x = x + g2[:, None, :] * y

    return rms_norm(x, g_final)
```

```python
import numpy as np
from contextlib import ExitStack
import concourse.bass as bass
import concourse.mybir as mybir
from concourse.bass_utils import run_bass_kernel_spmd

F32 = mybir.dt.float32
BF16 = mybir.dt.bfloat16
AF = mybir.ActivationFunctionType
ALU = mybir.AluOpType

D = 1024
DFF = 4096
DIN = 2560
NS = 5
TILE = 512
EPS = 1e-6


class _Op:
    __slots__ = ("idx", "eng", "fn", "dma_sem", "dma_cnt", "pos", "waits", "signal", "know", "know_dma", "sigcnt")


class Sched:
    ENG = ("pe", "act", "dve", "pool", "sp")

    def __init__(self):
        self.ops = []
        self.by_eng = {e: [] for e in self.ENG}
        self.last_w = {}
        self.readers = {}
        self.known = {e: {p: -1 for p in self.ENG} for e in self.ENG}
        self.known_dma = {e: {} for e in self.ENG}
        self.dma_counts = {}
        self.last_dma = {}

    def add(self, eng, fn, reads=(), writes=(), dma_sem=None):
        op = _Op()
        op.idx = len(self.ops); op.eng = eng; op.fn = fn; op.dma_sem = dma_sem
        op.signal = False; op.waits = []
        is_dma = dma_sem is not None
        deps = {}
        for r in reads:
            w = self.last_w.get(r)
            if w is not None:
                deps[w] = True
        for r in writes:
            w = self.last_w.get(r)
            if w is not None:
                deps.setdefault(w, False)
            for q in self.readers.get(r, ()):
                deps.setdefault(q, False)
        if is_dma and dma_sem in self.last_dma:
            deps.setdefault(self.last_dma[dma_sem], False)
        known = self.known[eng]; kd = self.known_dma[eng]
        for pidx in sorted(deps):
            p = self.ops[pidx]
            raw = deps[pidx]
            if p.dma_sem is not None:
                if kd.get(p.dma_sem, 0) >= p.dma_cnt:
                    continue
                op.waits.append(("dma", p.dma_sem, p.dma_cnt))
                self._merge(eng, p)
            else:
                if p.eng == eng and eng == "pe" and not is_dma and not raw:
                    continue
                if known[p.eng] >= p.pos:
                    continue
                op.waits.append(("eng", pidx))
                p.signal = True
                self._merge(eng, p)
        if is_dma:
            prev = self.dma_counts.get(dma_sem, 0)
            assert kd.get(dma_sem, 0) >= prev
            op.dma_cnt = prev + 16
            self.dma_counts[dma_sem] = op.dma_cnt
            self.last_dma[dma_sem] = op.idx
            op.pos = -1
            op.know = dict(known); op.know_dma = dict(kd)
            op.know_dma[dma_sem] = op.dma_cnt
        else:
            op.pos = self._npos(eng)
            op.know = dict(known); op.know[eng] = op.pos
            op.know_dma = dict(kd)
        self.by_eng[eng].append(op)
        self.ops.append(op)
        for r in reads:
            self.readers.setdefault(r, []).append(op.idx)
        for r in writes:
            self.last_w[r] = op.idx
            self.readers[r] = []
        return op

    def _npos(self, eng):
        c = getattr(self, "_cnt", None)
        if c is None:
            c = self._cnt = {e: 0 for e in self.ENG}
        v = c[eng]; c[eng] += 1
        return v

    def _merge(self, eng, p):
        known = self.known[eng]; kd = self.known_dma[eng]
        for k, v in p.know.items():
            if v > known[k]:
                known[k] = v
        for k, v in p.know_dma.items():
            if v > kd.get(k, 0):
                kd[k] = v

    def finalize(self):
        for e in self.ENG:
            c = 0
            for op in self.by_eng[e]:
                if op.dma_sem is None and op.signal:
                    c += 1
                    op.sigcnt = c

    def emit_engine(self, e, h, eng_sems, dma_sems):
        for op in self.by_eng[e]:
            for w in op.waits:
                if w[0] == "dma":
                    h.wait_ge(dma_sems[w[1]], w[2])
                else:
                    p = self.ops[w[1]]
                    h.wait_ge(eng_sems[p.eng], p.sigcnt)
            ins = op.fn(h)
            if op.dma_sem is not None:
                ins.then_inc(dma_sems[op.dma_sem], 16)
            elif op.signal:
                ins.then_inc(eng_sems[e], 1)


def build(T=4096):
    NT = T // TILE
    NTILES = 2 * NT
    import os as _os
    NTILES = min(NTILES, int(_os.environ.get('NTILES_LIMIT', '1000')))
    STAGE = int(_os.environ.get('STAGE', '99'))
    nc = bass.Bass("TRN2", target_bir_lowering=False)

    def dram_in(name, shape, dt=F32):
        return nc.dram_tensor(name, shape, dt, kind="ExternalInput").ap()

    x_d = dram_in("x", [2 * T, D])
    wada_d = dram_in("w_ada", [D, 6 * D])
    win_d = dram_in("w_in", [D, DIN])
    wout_d = dram_in("w_out", [D, D])
    w1_d = dram_in("w_mlp1", [D, DFF])
    w2_d = dram_in("w_mlp2", [DFF, D])
    id_d = dram_in("ident", [128, 128])
    smallt_d = dram_in("smallt", [128, 4, 37])
    ct8_d = dram_in("ct8", [128, 8, 8])
    bbg_d = dram_in("bbg", [128, 3, D])
    out_d = nc.dram_tensor("out", [2 * T, D], F32, kind="ExternalOutput").ap()
    wins_d = nc.dram_tensor("wins", [10, 128, 2048], BF16, kind="Internal").ap()
    w1s_d = nc.dram_tensor("w1s", [16, 128, 2048], BF16, kind="Internal").ap()
    w2s_d = nc.dram_tensor("w2s", [16, 128, 2048], BF16, kind="Internal").ap()
    gsc_d = nc.dram_tensor("gsc", [128, 2048], F32, kind="Internal").ap()

    S = Sched()
    add = S.add

    with ExitStack() as es:
        def sb(name, shape, dt):
            return es.enter_context(nc.sbuf_tensor(name, shape, dt))

        WOUT = sb("WOUT", [128, 8, 1024], BF16)
        DIAG = sb("DIAG", [128, 124, 128], BF16)
        XB = sb("XB", [128, 8, 1024], F32)
        HB = sb("HB", [128, 4, 1024], BF16)
        HT = sb("HT", [128, 8, 512], BF16)
        MIX = sb("MIX", [128, 8, 512], BF16)
        U = sb("U", [128, 32, 512], BF16)
        APRE = sb("APRE", [128, 4, 544], BF16)
        CZ = sb("CZ", [128, 4, 516], F32)
        TMP = sb("TMP", [128, 8, 512], F32)
        MEAN = sb("MEAN", [128, 512], F32)
        LNV = sb("LNV", [128, 512], F32)
        RING = sb("RING", [128, NS, 2048], BF16)
        G = sb("G", [128, 3, 1024], F32)
        JUNK = sb("JUNK", [128, 1024], BF16)
        IDENTF = sb("IDENTF", [128, 128], F32)
        IDENTB = sb("IDENTB", [128, 128], BF16)
        ONES = sb("ONES", [128, 128], BF16)
        ONE1 = sb("ONE1", [128, 128], BF16)
        SMALLT = sb("SMALLT", [128, 4, 37], F32)
        MODS = sb("MODS", [128, 4, 8, 2], F32)
        CT8 = sb("CT8", [128, 8, 8], F32)
        SGC = sb("SGC", [128, 8, 2], F32)
        CACTF = sb("CACTF", [128, 8, 2], F32)
        CACTB = sb("CACTB", [128, 8, 16], BF16)
        STAT = sb("STAT", [128, 2, 3, 4, 4], F32)
        Uflat = U[:].rearrange("p f t -> p (f t)")
        ACV = Uflat[:, 0:4096].bitcast(F32).rearrange("p (c t) -> p c t", c=4)
        SQ = Uflat[:, 4096:6144].rearrange("p (c t) -> p c t", c=4)
        ACVB = Uflat[:, 6144:8192].rearrange("p (c t) -> p c t", c=4)
        BB = Uflat[:, 11264:15360].bitcast(F32).rearrange("p (v d) -> p v d", v=2)
        CB = HB[:].rearrange("p j d -> p (j d)")[:, 0:2048].rearrange("p (b k m) -> p b k m", b=2, k=8)
        GT1 = XB[:, 4:6, :]

        PS = [es.enter_context(nc.psum_tensor(f"ps{i}", [128, 512], F32)) for i in range(8)]
        PT = [PS[0][:].bitcast(BF16)[:, 0:512], PS[7][:].bitcast(BF16)[:, 0:512]]
        PTK = [("PS", 0), ("PS", 7)]

        eng_sems = {e: es.enter_context(nc.semaphore("s_" + e)) for e in S.ENG}
        dma_names = [f"xl{i}" for i in range(8)] + [f"st{i}" for i in range(8)] + [f"r{i}" for i in range(NS)] + \
                    [f"p{i}" for i in range(16)] + [f"q{i}" for i in range(8)] + [f"wa{i}" for i in range(NS)]
        dma_sems = {n: es.enter_context(nc.semaphore("d_" + n)) for n in dma_names}

        prep_ctr = [0]

        def psem():
            n = prep_ctr[0] % 16
            prep_ctr[0] += 1
            return f"p{n}"

        qctr = [0]

        def qsem():
            n = qctr[0] % 8
            qctr[0] += 1
            return f"q{n}"

        gen_ctr = [0]

        def gen_bank():
            b = 1 + gen_ctr[0] % 3
            gen_ctr[0] += 1
            return b

        add("sp", lambda h: h.dma_start(out=IDENTF[:], in_=id_d[:, :]), writes=["IDENTF"], dma_sem=qsem())
        add("sp", lambda h: h.dma_start(out=SMALLT[:], in_=smallt_d[:, :, :]), writes=["SMALLT"], dma_sem=qsem())
        add("sp", lambda h: h.dma_start(out=CT8[:], in_=ct8_d[:, :, :]), writes=["CT8"], dma_sem=qsem())
        add("sp", lambda h: h.dma_start(out=BB[:, 0, :], in_=bbg_d[:, 0, :]), writes=["BB0"], dma_sem=qsem())
        add("sp", lambda h: h.dma_start(out=BB[:, 1, :], in_=bbg_d[:, 1, :]), writes=["BB1"], dma_sem=qsem())
        add("sp", lambda h: h.dma_start(out=G[:, 2, :], in_=bbg_d[:, 2, :]), writes=[("G", 2)], dma_sem=qsem())

        add("dve", lambda h: h.tensor_copy(IDENTB[:], IDENTF[:]), reads=["IDENTF"], writes=["IDENTB"])
        add("dve", lambda h: h.memset(ONES[:], 1.0 / 512.0), writes=["ONES"])
        add("dve", lambda h: h.memset(ONE1[:], 1.0), writes=["ONE1"])
        add("dve", lambda h: h.memset(STAT[:].rearrange("p a b c d -> p (a b c d)"), 0.0), writes=[("STAT", p, n, k, j) for p in range(2) for n in range(3) for k in range(4) for j in range(4)])
        add("dve", lambda h: h.memset(CACTB[:].rearrange("p k b -> p (k b)"), 0.0), writes=["CACTB"])
        add("act", lambda h: h.activation(out=SGC[:], in_=CT8[:, :, 0:2], func=AF.Sigmoid), reads=["CT8"], writes=["SGC"])
        add("dve", lambda h: h.tensor_tensor(CACTF[:], CT8[:, :, 0:2], SGC[:], ALU.mult), reads=["CT8", "SGC"], writes=["CACTF"])
        add("dve", lambda h: h.tensor_copy(CACTB[:, :, 0:2], CACTF[:]), reads=["CACTF", "CACTB"], writes=["CACTB"])
        for b in range(2):
            for k in range(8):
                add("dve", lambda h, b=b, k=k: h.tensor_scalar(CB[:, b, k, :], ONE1[:], CACTF[:, k, b:b + 1], None, ALU.mult),
                    reads=["ONE1", "CACTF"], writes=[("HB", b)])

        for c4 in range(4):
            for k in range(31):
                e = "dve"
                add(e, lambda h, c4=c4, k=k: h.tensor_scalar(DIAG[:, c4 * 31 + k, :], IDENTB[:], SMALLT[:, c4, k:k + 1], None, ALU.mult),
                    reads=["IDENTB", "SMALLT"], writes=[("DIAG", c4, k)])

        ring_n = [0]
        PMOD = PS[1][:, 0:512].rearrange("p (v k b) -> p v k b", v=4, k=8)
        FM_V = {0: 0, 1: 1, 3: 2, 4: 3}
        for v in (0, 1, 3, 4, 2, 5):
            for q in range(4):
                slot = ring_n[0] % NS
                ring_n[0] += 1
                col0 = v * 1024 + q * 256
                rs = RING[:, slot, :].rearrange("p (k f) -> p k f", k=8)
                add("pool", lambda h, rs=rs, col0=col0: h.dma_start(
                    out=rs, in_=wada_d[:, col0:col0 + 256].rearrange("(k p) f -> p k f", p=128)),
                    writes=[("RING", slot)], dma_sem=f"wa{slot}")
                if v in FM_V:
                    vi = FM_V[v]
                    for oc in range(2):
                        ko = q * 2 + oc
                        for kin in range(8):
                            add("pe", lambda h, rs=rs, vi=vi, ko=ko, oc=oc, kin=kin: h.matmul(
                                PMOD[:, vi, ko, :], lhsT=rs[:, kin, oc * 128:(oc + 1) * 128], rhs=CACTB[:, kin, :],
                                start=(kin == 0), stop=(kin == 7)),
                                reads=[("RING", slot), "CACTB"], writes=[("PS", 1)])
                else:
                    gi = 0 if v == 2 else 1
                    for b in range(2):
                        bank = gen_bank()
                        if bank == 1:
                            bank = gen_bank()
                        for kin in range(8):
                            add("pe", lambda h, rs=rs, b=b, kin=kin, bank=bank: h.matmul(
                                PS[bank][:, 0:256], lhsT=CB[:, b, kin, :], rhs=rs[:, kin, :],
                                start=(kin == 0), stop=(kin == 7)),
                                reads=[("RING", slot), ("HB", b)], writes=[("PS", bank)])
                        dst = G[:, gi, q * 256:(q + 1) * 256] if b == 0 else GT1[:, gi, q * 256:(q + 1) * 256]
                        dkey = ("G", gi) if b == 0 else ("XB", 4 + gi, q // 2)
                        add("dve", lambda h, dst=dst, bank=bank, gi=gi, q=q: h.tensor_tensor(
                            dst, PS[bank][:, 0:256], BB[:, gi, q * 256:(q + 1) * 256], ALU.add),
                            reads=[("PS", bank), f"BB{gi}"], writes=[dkey])
        for v, vi in FM_V.items():
            for b in range(2):
                add("dve", lambda h, v=v, vi=vi, b=b: h.tensor_tensor(MODS[:, vi, :, b], PMOD[:, vi, :, b], CT8[:, :, 2 + v], ALU.add),
                    reads=[("PS", 1), "CT8"], writes=[("MODS", vi, b)])
        for vi in (1, 3):
            add("dve", lambda h, vi=vi: h.tensor_scalar(MODS[:, vi, :, :], MODS[:, vi, :, :], 1.0, None, ALU.add),
                reads=[("MODS", vi, 0), ("MODS", vi, 1)], writes=[("MODS", vi, 0), ("MODS", vi, 1)])
        add("sp", lambda h: h.dma_start(out=gsc_d[:, :], in_=GT1.rearrange("p v d -> p (v d)")),
            reads=[("XB", 4, 0), ("XB", 4, 1), ("XB", 5, 0), ("XB", 5, 1)], writes=["GSC"], dma_sem=qsem())

        add("pool", lambda h: h.dma_start(out=WOUT[:], in_=wout_d[:, :].rearrange("(k p) d -> p k d", p=128)),
            writes=["WOUT"], dma_sem=psem())
        E_ORDER = []
        for c4 in range(4):
            E_ORDER += [("gate", c4, 512 + c4 * 128), ("val", c4, c4 * 128)]
        E_ORDER += [("scc", 0, 1536), ("sch", 0, 2048), ("scc", 1, 1536 + 128), ("sch", 1, 2048 + 128),
                    ("scb", 0, 1024), ("scb", 1, 1024 + 128),
                    ("scc", 2, 1536 + 256), ("sch", 2, 2048 + 256), ("scc", 3, 1536 + 384), ("sch", 3, 2048 + 384),
                    ("scb", 2, 1024 + 256), ("scb", 3, 1024 + 384)]
        for g in range(10):
            for hh in range(2):
                col = E_ORDER[2 * g + hh][2]
                dst = wins_d[g, :, :].rearrange("p (k f) -> p k f", k=8)[:, :, hh * 128:(hh + 1) * 128]
                add("pool", lambda h, dst=dst, col=col: h.dma_start(
                    out=dst, in_=win_d[:, col:col + 128].rearrange("(k p) f -> p k f", p=128)),
                    writes=[("WINS", g, hh)], dma_sem=psem())
        for g in range(16):
            dst = w1s_d[g, :, :].rearrange("p (k f) -> p k f", k=8)
            add("pool", lambda h, dst=dst, g=g: h.dma_start(
                out=dst, in_=w1_d[:, g * 256:(g + 1) * 256].rearrange("(k p) f -> p k f", p=128)),
                writes=[("W1S", g)], dma_sem=psem())
        for hh in range(2):
            for g in range(8):
                dst = w2s_d[hh * 8 + g, :, :].rearrange("p (j d) -> p j d", j=4)
                add("pool", lambda h, dst=dst, g=g, hh=hh: h.dma_start(
                    out=dst, in_=w2_d[g * 512:(g + 1) * 512, hh * 512:(hh + 1) * 512].rearrange("(j p) d -> p j d", p=128)),
                    writes=[("W2S", hh * 8 + g)], dma_sem=psem())

        stream = []
        for tg in range(NTILES):
            stream += [("win", g) for g in range(10)] + [("w1", g) for g in range(16)] + [("w2", g) for g in range(16)]
        s_issued = [0]
        s_base = ring_n[0]

        def issue_upto(n):
            n = min(n, len(stream) - 1)
            while s_issued[0] <= n:
                m = s_issued[0]
                kind, g = stream[m]
                slot = (s_base + m) % NS
                if kind == "win":
                    src, keys = wins_d[g, :, :], [("WINS", g, 0), ("WINS", g, 1)]
                elif kind == "w1":
                    src, keys = w1s_d[g, :, :], [("W1S", g)]
                else:
                    src, keys = w2s_d[g, :, :], [("W2S", g)]
                add("sp", lambda h, slot=slot, src=src: h.dma_start(out=RING[:, slot, :], in_=src),
                    reads=keys, writes=[("RING", slot)], dma_sem=f"r{slot}")
                s_issued[0] += 1

        s_pos = [0]

        def next_group():
            m = s_pos[0]
            s_pos[0] += 1
            issue_upto(m + NS - 1)
            return (s_base + m) % NS

        def load_x(tg):
            b, i = divmod(tg, NT)
            p = tg % 2
            for j in range(4):
                r0 = b * T + i * TILE + j * 128
                add("sp", lambda h, p=p, j=j, r0=r0: h.dma_start(out=XB[:, p * 4 + j, :], in_=x_d[r0:r0 + 128, :]),
                    writes=[("XB", p * 4 + j, 0), ("XB", p * 4 + j, 1)], dma_sem=f"xl{p * 4 + j}")

        def stat_keys(p, n, k, js):
            return [("STAT", p, n, k, j) for j in js]

        def rstd_chain(p, n, js):
            j0, j1 = js[0], js[-1] + 1
            add("dve", lambda h: h.tensor_scalar(STAT[:, p, n, 1, j0:j1], STAT[:, p, n, 0, j0:j1], 1.0 / D, EPS, ALU.mult, ALU.add),
                reads=stat_keys(p, n, 0, js), writes=stat_keys(p, n, 1, js))
            add("act", lambda h: h.activation(out=STAT[:, p, n, 2, j0:j1], in_=STAT[:, p, n, 1, j0:j1], func=AF.Sqrt),
                reads=stat_keys(p, n, 1, js), writes=stat_keys(p, n, 2, js))
            add("dve", lambda h: h.reciprocal(STAT[:, p, n, 3, j0:j1], STAT[:, p, n, 2, j0:j1]),
                reads=stat_keys(p, n, 2, js), writes=stat_keys(p, n, 3, js))

        def sumsq(p, n, j):
            pj = p * 4 + j
            add("dve", lambda h: h.memset(STAT[:, p, n, 0, j:j + 1], 0.0), writes=stat_keys(p, n, 0, [j]))
            add("act", lambda h: h.activation(out=JUNK[:], in_=XB[:, pj, :], func=AF.Square, accum_out=STAT[:, p, n, 0, j:j + 1]),
                reads=[("XB", pj, 0), ("XB", pj, 1)] + stat_keys(p, n, 0, [j]), writes=["JUNK"] + stat_keys(p, n, 0, [j]))

        def scale_h(p, n, j):
            pj = p * 4 + j
            add("act", lambda h: h.activation(out=HB[:, j, :], in_=XB[:, pj, :], func=AF.Identity, scale=STAT[:, p, n, 3, j:j + 1]),
                reads=[("XB", pj, 0), ("XB", pj, 1)] + stat_keys(p, n, 3, [j]), writes=[("HB", j)])

        def norm1(tg):
            p = tg % 2
            for j in range(4):
                sumsq(p, 0, j)
            rstd_chain(p, 0, [0, 1, 2, 3])
            for j in range(4):
                scale_h(p, 0, j)

        def transposes(b, vi_scale, vi_shift):
            for k in range(8):
                slot = k % 2
                for j in range(4):
                    add("pe", lambda h, slot=slot, j=j, k=k: h.transpose(PT[slot][:, j * 128:(j + 1) * 128], HB[:, j, k * 128:(k + 1) * 128], IDENTB[:]),
                        reads=[("HB", j), "IDENTB"], writes=[PTK[slot]])
                rk = [PTK[slot], ("MODS", vi_scale, b), ("MODS", vi_shift, b)]
                if k % 2 == 0:
                    add("dve", lambda h, slot=slot, k=k: h.tensor_scalar(HT[:, k, :], PT[slot], MODS[:, vi_scale, k, b:b + 1], MODS[:, vi_shift, k, b:b + 1], ALU.mult, ALU.add),
                        reads=rk, writes=[("HT", k)])
                else:
                    add("act", lambda h, slot=slot, k=k: h.activation(out=HT[:, k, :], in_=PT[slot], func=AF.Identity,
                                                                      scale=MODS[:, vi_scale, k, b:b + 1], bias=MODS[:, vi_shift, k, b:b + 1]),
                        reads=rk, writes=[("HT", k)])

        HTK = [("HT", k) for k in range(8)]

        load_x(0)
        if NTILES > 1:
            load_x(1)
        norm1(0)
        for tg in range(NTILES):
            b, i = divmod(tg, NT)
            p = tg % 2
            if tg >= 1 and tg + 1 < NTILES:
                load_x(tg + 1)
            if i == 0:
                if b == 1:
                    add("sp", lambda h: h.dma_start(out=G[:, 0:2, :].rearrange("p v d -> p (v d)"), in_=gsc_d[:, :]),
                        reads=["GSC"], writes=[("G", 0), ("G", 1)], dma_sem=qsem())
                for c4 in range(4):
                    add("dve", lambda h, c4=c4: h.memset(CZ[:, c4, 0:2], 0.0), writes=[("CZ", c4)])
                    add("dve", lambda h, c4=c4: h.memset(APRE[:, c4, 0:30], 0.0), writes=[("APRE", c4)])

            transposes(b, 1, 0)

            if STAGE < 2:
                continue
            for g in range(10):
                slot = next_group()
                rs = RING[:, slot, :].rearrange("p (k f) -> p k f", k=8)
                for hh in range(2):
                    kind, c4, _ = E_ORDER[2 * g + hh]
                    bank = gen_bank()
                    for k in range(8):
                        add("pe", lambda h, rs=rs, hh=hh, k=k, bank=bank: h.matmul(
                            PS[bank][:], lhsT=rs[:, k, hh * 128:(hh + 1) * 128], rhs=HT[:, k, :], start=(k == 0), stop=(k == 7)),
                            reads=[("RING", slot), ("HT", k)], writes=[("PS", bank)])
                    if kind == "gate":
                        add("act", lambda h, bank=bank, c4=c4: h.activation(out=TMP[:, c4 % 2, :], in_=PS[bank][:], func=AF.Sigmoid),
                            reads=[("PS", bank)], writes=[("TMP", c4 % 2)])
                    elif kind == "val":
                        add("dve", lambda h, bank=bank, c4=c4: h.tensor_tensor(APRE[:, c4, 30:542], PS[bank][:], TMP[:, c4 % 2, :], ALU.mult),
                            reads=[("PS", bank), ("TMP", c4 % 2)], writes=[("APRE", c4)])
                    elif kind == "scc":
                        add("act", lambda h, bank=bank, c4=c4: h.activation(out=TMP[:, 2 + c4 % 2, :], in_=PS[bank][:], func=AF.Identity),
                            reads=[("PS", bank)], writes=[("TMP", 2 + c4 % 2)])
                    elif kind == "sch":
                        t3 = 4 + c4 % 2
                        add("dve", lambda h, bank=bank, c4=c4: h.tensor_tensor(CZ[:, c4, 2:514], PS[bank][:], TMP[:, 2 + c4 % 2, :], ALU.mult),
                            reads=[("PS", bank), ("TMP", 2 + c4 % 2)], writes=[("CZ", c4)])
                        add("dve", lambda h, c4=c4, t3=t3: h.tensor_scalar(TMP[:, t3, :], CZ[:, c4, 0:512], SMALLT[:, c4, 31:32], None, ALU.mult),
                            reads=[("CZ", c4), "SMALLT"], writes=[("TMP", t3)])
                        add("dve", lambda h, c4=c4, t3=t3: h.scalar_tensor_tensor(TMP[:, t3, :], CZ[:, c4, 1:513], SMALLT[:, c4, 32:33], TMP[:, t3, :], ALU.mult, ALU.add),
                            reads=[("CZ", c4), "SMALLT", ("TMP", t3)], writes=[("TMP", t3)])
                        add("dve", lambda h, c4=c4, t3=t3: h.scalar_tensor_tensor(TMP[:, t3, :], CZ[:, c4, 2:514], SMALLT[:, c4, 33:34], TMP[:, t3, :], ALU.mult, ALU.add),
                            reads=[("CZ", c4), "SMALLT", ("TMP", t3)], writes=[("TMP", t3)])
                        add("dve", lambda h, c4=c4: h.tensor_copy(CZ[:, c4, 0:2], CZ[:, c4, 512:514]),
                            reads=[("CZ", c4)], writes=[("CZ", c4)])
                    else:
                        t3 = 4 + c4 % 2
                        add("dve", lambda h, bank=bank, c4=c4, t3=t3: h.tensor_tensor(MIX[:, 4 + c4, :], PS[bank][:], TMP[:, t3, :], ALU.mult),
                            reads=[("PS", bank), ("TMP", t3)], writes=[("MIX", 4 + c4)])

            if STAGE < 3:
                continue
            for c4 in range(4):
                bank = gen_bank()
                for k in range(31):
                    add("pe", lambda h, c4=c4, k=k, bank=bank: h.matmul(
                        PS[bank][:], lhsT=DIAG[:, c4 * 31 + k, :], rhs=APRE[:, c4, k:k + 512], start=(k == 0), stop=(k == 30)),
                        reads=[("DIAG", c4, k), ("APRE", c4)], writes=[("PS", bank)])
                add("act", lambda h, c4=c4, bank=bank: h.activation(out=ACV[:, c4, :], in_=PS[bank][:], func=AF.Identity, bias=SMALLT[:, c4, 34:35]),
                    reads=[("PS", bank), "SMALLT"], writes=[("ACV", c4)])
                add("act", lambda h, c4=c4, bank=bank: h.activation(out=SQ[:, c4, :], in_=PS[bank][:], func=AF.Square, bias=SMALLT[:, c4, 34:35]),
                    reads=[("PS", bank), "SMALLT"], writes=[("SQ", c4)])
                add("dve", lambda h, c4=c4: h.tensor_copy(ACVB[:, c4, :], ACV[:, c4, :]),
                    reads=[("ACV", c4)], writes=[("ACVB", c4)])
                add("pool", lambda h, c4=c4: h.tensor_copy(APRE[:, c4, 0:30], APRE[:, c4, 512:542]),
                    reads=[("APRE", c4)], writes=[("APRE", c4)])
            if STAGE < 4:
                continue
            for c4 in range(4):
                add("pe", lambda h, c4=c4: h.matmul(PS[4][:], lhsT=ONES[:], rhs=ACVB[:, c4, :], start=(c4 == 0), stop=(c4 == 3)),
                    reads=["ONES", ("ACVB", c4)], writes=[("PS", 4)])
            for c4 in range(4):
                add("pe", lambda h, c4=c4: h.matmul(PS[5][:], lhsT=ONES[:], rhs=SQ[:, c4, :], start=(c4 == 0), stop=(c4 == 3)),
                    reads=["ONES", ("SQ", c4)], writes=[("PS", 5)])
            add("act", lambda h: h.activation(out=MEAN[:], in_=PS[4][:], func=AF.Identity), reads=[("PS", 4)], writes=["MEAN"])
            add("act", lambda h: h.activation(out=LNV[:], in_=PS[4][:], func=AF.Square), reads=[("PS", 4)], writes=["LNV"])
            add("dve", lambda h: h.scalar_tensor_tensor(LNV[:], PS[5][:], EPS, LNV[:], ALU.add, ALU.subtract),
                reads=[("PS", 5), "LNV"], writes=["LNV"])
            add("dve", lambda h: h.tensor_scalar(LNV[:], LNV[:], EPS * 0.5, None, ALU.max), reads=["LNV"], writes=["LNV"])
            add("act", lambda h: h.activation(out=LNV[:], in_=LNV[:], func=AF.Sqrt), reads=["LNV"], writes=["LNV"])
            add("dve", lambda h: h.reciprocal(LNV[:], LNV[:]), reads=["LNV"], writes=["LNV"])
            for c4 in range(4):
                tb = 6 + c4 % 2
                sg = c4 % 2
                add("dve", lambda h, c4=c4, tb=tb: h.tensor_tensor(TMP[:, tb, :], ACV[:, c4, :], MEAN[:], ALU.subtract),
                    reads=[("ACV", c4), "MEAN"], writes=[("TMP", tb)])
                add("dve", lambda h, tb=tb: h.tensor_tensor(TMP[:, tb, :], TMP[:, tb, :], LNV[:], ALU.mult),
                    reads=[("TMP", tb), "LNV"], writes=[("TMP", tb)])
                add("act", lambda h, c4=c4, tb=tb, sg=sg: h.activation(out=TMP[:, sg, :], in_=TMP[:, tb, :], func=AF.Sigmoid,
                                                                       scale=SMALLT[:, c4, 35:36], bias=SMALLT[:, c4, 36:37]),
                    reads=[("TMP", tb), "SMALLT"], writes=[("TMP", sg)])
                add("dve", lambda h, c4=c4, tb=tb: h.tensor_scalar(TMP[:, tb, :], TMP[:, tb, :], SMALLT[:, c4, 35:36], SMALLT[:, c4, 36:37], ALU.mult, ALU.add),
                    reads=[("TMP", tb), "SMALLT"], writes=[("TMP", tb)])
                add("dve", lambda h, c4=c4, tb=tb, sg=sg: h.tensor_tensor(MIX[:, c4, :], TMP[:, tb, :], TMP[:, sg, :], ALU.mult),
                    reads=[("TMP", tb), ("TMP", sg)], writes=[("MIX", c4)])

            if STAGE < 5:
                continue
            for j in range(4):
                pj = p * 4 + j
                for hh in range(2):
                    bank = gen_bank()
                    korder = [4, 5, 6, 7, 0, 1, 2, 3]
                    for n, k in enumerate(korder):
                        add("pe", lambda h, j=j, hh=hh, k=k, n=n, bank=bank: h.matmul(
                            PS[bank][:], lhsT=MIX[:, k, j * 128:(j + 1) * 128], rhs=WOUT[:, k, hh * 512:(hh + 1) * 512],
                            start=(n == 0), stop=(n == 7)),
                            reads=[("MIX", k), "WOUT"], writes=[("PS", bank)])
                    yt = 2 + hh
                    add("dve", lambda h, bank=bank, hh=hh, yt=yt: h.tensor_tensor(TMP[:, yt, :], PS[bank][:], G[:, 0, hh * 512:(hh + 1) * 512], ALU.mult),
                        reads=[("PS", bank), ("G", 0)], writes=[("TMP", yt)])
                    add("pool", lambda h, pj=pj, hh=hh, yt=yt: h.tensor_tensor(XB[:, pj, hh * 512:(hh + 1) * 512], XB[:, pj, hh * 512:(hh + 1) * 512], TMP[:, yt, :], ALU.add),
                        reads=[("XB", pj, hh), ("TMP", yt)], writes=[("XB", pj, hh)])
                sumsq(p, 1, j)
                rstd_chain(p, 1, [j])
                scale_h(p, 1, j)

            if STAGE < 6:
                continue
            transposes(b, 3, 2)
            if tg + 1 < NTILES:
                norm1(tg + 1)

            if STAGE < 7:
                continue
            for g in range(16):
                slot = next_group()
                rs = RING[:, slot, :].rearrange("p (k f) -> p k f", k=8)
                for hh in range(2):
                    f = 2 * g + hh
                    bank = gen_bank()
                    for k in range(8):
                        add("pe", lambda h, rs=rs, hh=hh, k=k, bank=bank: h.matmul(
                            PS[bank][:], lhsT=rs[:, k, hh * 128:(hh + 1) * 128], rhs=HT[:, k, :], start=(k == 0), stop=(k == 7)),
                            reads=[("RING", slot), ("HT", k)], writes=[("PS", bank)])
                    r = f % 3
                    add("act", lambda h, bank=bank, r=r: h.activation(out=TMP[:, r, :], in_=PS[bank][:], func=AF.Relu),
                        reads=[("PS", bank)], writes=[("TMP", r)])
                    if r == 0:
                        add("act", lambda h, f=f, r=r: h.activation(out=U[:, f, :], in_=TMP[:, r, :], func=AF.Square),
                            reads=[("TMP", r)], writes=[("U", f)])
                    elif r == 1:
                        add("dve", lambda h, f=f, r=r: h.tensor_tensor(U[:, f, :], TMP[:, r, :], TMP[:, r, :], ALU.mult),
                            reads=[("TMP", r)], writes=[("U", f)])
                    else:
                        add("pool", lambda h, f=f, r=r: h.tensor_tensor(U[:, f, :], TMP[:, r, :], TMP[:, r, :], ALU.mult),
                            reads=[("TMP", r)], writes=[("U", f)])

            if STAGE < 8:
                continue
            for hh in range(2):
                for g in range(8):
                    slot = next_group()
                    rs = RING[:, slot, :].rearrange("p (j d) -> p j d", j=4)
                    for fl in range(4):
                        f = g * 4 + fl
                        for j in range(4):
                            add("pe", lambda h, rs=rs, fl=fl, f=f, j=j: h.matmul(
                                PS[4 + j][:], lhsT=U[:, f, j * 128:(j + 1) * 128], rhs=rs[:, fl, :], start=(f == 0), stop=(f == 31)),
                                reads=[("RING", slot), ("U", f)], writes=[("PS", 4 + j)])
                for j in range(4):
                    pj = p * 4 + j
                    yt = 4 + j % 2
                    add("dve", lambda h, j=j, hh=hh, yt=yt: h.tensor_tensor(TMP[:, yt, :], PS[4 + j][:], G[:, 1, hh * 512:(hh + 1) * 512], ALU.mult),
                        reads=[("PS", 4 + j), ("G", 1)], writes=[("TMP", yt)])
                    add("pool", lambda h, pj=pj, hh=hh, yt=yt: h.tensor_tensor(XB[:, pj, hh * 512:(hh + 1) * 512], XB[:, pj, hh * 512:(hh + 1) * 512], TMP[:, yt, :], ALU.add),
                        reads=[("XB", pj, hh), ("TMP", yt)], writes=[("XB", pj, hh)])
                    if hh == 1:
                        sumsq(p, 2, j)
                        rstd_chain(p, 2, [j])
                        add("dve", lambda h, pj=pj, j=j, p=p: h.scalar_tensor_tensor(XB[:, pj, :], XB[:, pj, :], STAT[:, p, 2, 3, j:j + 1], G[:, 2, :], ALU.mult, ALU.mult),
                            reads=[("XB", pj, 0), ("XB", pj, 1), ("G", 2)] + stat_keys(p, 2, 3, [j]), writes=[("XB", pj, 0), ("XB", pj, 1)])
                        r0 = b * T + i * TILE + j * 128
                        add("sp", lambda h, pj=pj, r0=r0: h.dma_start(out=out_d[r0:r0 + 128, :], in_=XB[:, pj, :]),
                            reads=[("XB", pj, 0), ("XB", pj, 1)], writes=[("OUT", tg, j)], dma_sem=f"st{pj}")

        add("sp", lambda h: h.nop(), reads=[("OUT", tg, j) for tg in range(NTILES) for j in range(4)])

        S.finalize()
        with nc.Block() as block:
            @block.tensor
            def _(h):
                S.emit_engine("pe", h, eng_sems, dma_sems)

            @block.scalar
            def _(h):
                S.emit_engine("act", h, eng_sems, dma_sems)

            @block.vector
            def _(h):
                S.emit_engine("dve", h, eng_sems, dma_sems)

            @block.gpsimd
            def _(h):
                S.emit_engine("pool", h, eng_sems, dma_sems)

            @block.sync
            def _(h):
                S.emit_engine("sp", h, eng_sems, dma_sems)
    return nc


def make_in_maps(inputs, n_cores, T):
    f = lambda a: np.ascontiguousarray(np.asarray(a, dtype=np.float32))
    x = f(inputs["x"])
    c = f(inputs["c"])
    rows = np.concatenate([f(inputs["conf_dw_w"][0]), f(inputs["sc_conv_w"][0]), f(inputs["conf_dw_b"][0])[None],
                           f(inputs["conf_ln_g"][0])[None], f(inputs["conf_ln_b"][0])[None]], axis=0)
    smallt = np.ascontiguousarray(rows.reshape(37, 4, 128).transpose(2, 1, 0))
    bada = f(inputs["b_ada"][0]).reshape(6, D)
    gfin = f(inputs["g_final"]).reshape(1, D)
    bbg = np.ascontiguousarray(np.broadcast_to(np.stack([bada[2], bada[5], gfin[0]], axis=0)[None], (128, 3, D)))
    shared = {
        "w_ada": f(inputs["w_ada"][0]),
        "w_in": f(inputs["w_in"][0]),
        "w_out": f(inputs["w_out"][0]),
        "w_mlp1": f(inputs["w_mlp1"][0]),
        "w_mlp2": f(inputs["w_mlp2"][0]),
        "ident": np.eye(128, dtype=np.float32),
        "smallt": smallt,
        "bbg": bbg,
    }
    maps = []
    for i in range(n_cores):
        m = dict(shared)
        m["x"] = np.ascontiguousarray(x[2 * i:2 * i + 2].reshape(2 * T, D))
        rows2 = np.concatenate([c[2 * i:2 * i + 2], bada], axis=0)
        m["ct8"] = np.ascontiguousarray(rows2.reshape(8, 8, 128).transpose(2, 1, 0))
        maps.append(m)
    return maps


def kernel(**inputs):
    x = np.asarray(inputs["x"])
    B, T, _ = x.shape
    n_cores = B // 2
    nc = build(T)
    in_maps = make_in_maps(inputs, n_cores, T)
    res = run_bass_kernel_spmd(nc, in_maps, core_ids=list(range(n_cores)))
    out = np.concatenate([np.asarray(r["out"]).reshape(2, T, D) for r in res.results], axis=0)
    return out.astype(np.float32)
```
